# Optimizing a Trainium2 kernel written in Bass

```python
import jax, jax.numpy as jnp
from jax import lax
import numpy as np

D_MODEL = 4096
BATCH = 4
SEQ = 2048
DEPTH = 2
DEC_BATCH = 8
DEC_SEQ = 1
PAST_LEN = 16384
PAGE_SIZE = 128

HEAD_DIM = 128
N_GROUPS = 3
HEADS_PER_GROUP = D_MODEL // (4 * HEAD_DIM)
ATTN_HEADS = N_GROUPS * HEADS_PER_GROUP
QKV_DIM = ATTN_HEADS * HEAD_DIM
ATTN_OUT = HEADS_PER_GROUP * HEAD_DIM
GROUPS = ((128, 1), (512, 4), (2048, 16))
BLOCK = 128
CONV_DIM = D_MODEL
CONV_WIDTH = 31
ROPE_THETA = 10000.0
EPS = 1e-6
NEG_INF = -1e30
IN_SIZES = (CONV_DIM, CONV_DIM, CONV_DIM, QKV_DIM, QKV_DIM, QKV_DIM, ATTN_OUT, D_MODEL, D_MODEL)
N_IN = 3 * CONV_DIM + 3 * QKV_DIM + ATTN_OUT + 2 * D_MODEL

kernel_name = 'dilated_conformer_gated_hybrid_step'


def _rmsnorm(x, g):
    x32 = x.astype(jnp.float32)
    y = x32 * lax.rsqrt(jnp.mean(x32 * x32, axis=-1, keepdims=True) + EPS)
    return (y * g.astype(jnp.float32)).astype(x.dtype)


def _layernorm(x, g, b):
    x32 = x.astype(jnp.float32)
    xc = x32 - jnp.mean(x32, axis=-1, keepdims=True)
    var = jnp.mean(xc * xc, axis=-1, keepdims=True)
    return (xc * lax.rsqrt(var + EPS) * g.astype(jnp.float32) + b.astype(jnp.float32)).astype(x.dtype)


def _rotary(t, pos):
    half = HEAD_DIM // 2
    inv_freq = ROPE_THETA ** (-jnp.arange(half, dtype=jnp.float32) / half)
    ang = pos.astype(jnp.float32)[:, None] * inv_freq[None, :]
    cos = jnp.cos(ang)[None, :, None, :]
    sin = jnp.sin(ang)[None, :, None, :]
    t32 = t.astype(jnp.float32)
    t1, t2 = t32[..., :half], t32[..., half:]
    return jnp.concatenate([t1 * cos - t2 * sin, t2 * cos + t1 * sin], axis=-1).astype(t.dtype)


def _softmax_lse(s, mask):
    s = jnp.where(mask, s, NEG_INF)
    m = jnp.max(s, axis=-1, keepdims=True)
    e = jnp.exp(s - m)
    den = jnp.sum(e, axis=-1, keepdims=True)
    return e / den, (m + jnp.log(den))[..., 0]


def _dilated_prompt(q, k, v, window, dil):
    B, S, H, E = q.shape
    span = dil * BLOCK
    s_pad = -(-S // span) * span
    n_sub = s_pad // dil
    nb = n_sub // BLOCK

    def to_sub(t):
        t = jnp.pad(t, ((0, 0), (0, s_pad - S), (0, 0), (0, 0)))
        t = t.reshape(B, n_sub, dil, H, E).transpose(0, 2, 1, 3, 4)
        return t.reshape(B, dil, nb, BLOCK, H, E)

    def with_prev(t):
        prev = jnp.pad(t[:, :, :-1], ((0, 0), (0, 0), (1, 0), (0, 0), (0, 0), (0, 0)))
        return jnp.concatenate([prev, t], axis=3)

    def from_sub(t):
        rest = t.shape[4:]
        t = t.reshape((B, dil, n_sub) + rest).swapaxes(1, 2).reshape((B, s_pad) + rest)
        return t[:, :S]

    qb = to_sub(q)
    kk = with_prev(to_sub(k))
    vv = with_prev(to_sub(v))
    s = jnp.einsum('brnqhe,brnkhe->brnhqk', qb, kk).astype(jnp.float32)
    i = jnp.arange(BLOCK)[:, None]
    m = jnp.arange(2 * BLOCK)[None, :]
    dist = BLOCK + i - m
    band = (dist >= 0) & (dist <= window // dil)
    blk = jnp.arange(nb)
    mask = band[None] & ((blk[:, None, None] > 0) | (m >= BLOCK)[None])
    p, lse = _softmax_lse(s, mask[None, None, :, None])
    o = jnp.einsum('brnhqk,brnkhe->brnqhe', p.astype(vv.dtype), vv)
    return from_sub(o), from_sub(jnp.swapaxes(lse, 3, 4))


def _dilated_sample(q, k_ext, v_ext, n_past, window, dil):
    T = q.shape[1]
    idx = n_past + jnp.arange(T)[:, None] - dil * jnp.arange(window // dil + 1)[None, :]
    valid = idx >= 0
    idx = jnp.maximum(idx, 0)
    kg = k_ext[:, idx]
    vg = v_ext[:, idx]
    s = jnp.einsum('bthe,btkhe->bthk', q, kg).astype(jnp.float32)
    p, lse = _softmax_lse(s, valid[None, :, None, :])
    o = jnp.einsum('bthk,btkhe->bthe', p.astype(vg.dtype), vg)
    return o, lse


def _depthwise_causal(u_ext, w):
    return lax.conv_general_dilated(u_ext, w[:, None, :], window_strides=(1,), padding='VALID',
                                    dimension_numbers=('NWC', 'WIO', 'NWC'),
                                    feature_group_count=CONV_DIM)


def _layer(x, pos, conv_ctx, kv_bufs, norm_g, w_in, dw_w, dw_b, ln_g, ln_b, w_pc, w_pa, w_o):
    B, T, _ = x.shape
    h = _rmsnorm(x, norm_g)
    z = h @ w_in
    cuts = np.cumsum(IN_SIZES)[:-1].tolist()
    glu_a, glu_b, c_gate, q, k, v, a_gate, g_conv, g_attn = jnp.split(z, cuts, axis=-1)

    u = glu_a * jax.nn.sigmoid(glu_b)
    u_ext = jnp.concatenate([conv_ctx.astype(u.dtype), u], axis=1)
    c = _depthwise_causal(u_ext, dw_w) + dw_b
    c = jax.nn.silu(_layernorm(c, ln_g, ln_b)) * jax.nn.silu(c_gate)
    p_conv = c @ w_pc
    new_conv = u_ext[:, -(CONV_WIDTH - 1):]

    q = _rotary(q.reshape(B, T, ATTN_HEADS, HEAD_DIM), pos) * (HEAD_DIM ** -0.5)
    k = _rotary(k.reshape(B, T, ATTN_HEADS, HEAD_DIM), pos)
    q = q.reshape(B, T, N_GROUPS, HEADS_PER_GROUP, HEAD_DIM)
    k = k.reshape(B, T, N_GROUPS, HEADS_PER_GROUP, HEAD_DIM)
    v = v.reshape(B, T, N_GROUPS, HEADS_PER_GROUP, HEAD_DIM)
    outs, lses, new_kv = [], [], []
    for g, (window, dil) in enumerate(GROUPS):
        qg, kg, vg = q[:, :, g], k[:, :, g], v[:, :, g]
        if kv_bufs is None:
            o, lse = _dilated_prompt(qg, kg, vg, window, dil)
            keep = min(window, T)
            new_kv.append(jnp.stack([kg, vg], axis=2)[:, T - keep:])
        else:
            buf = kv_bufs[g].astype(kg.dtype)
            n_past = buf.shape[1]
            k_ext = jnp.concatenate([buf[:, :, 0], kg], axis=1)
            v_ext = jnp.concatenate([buf[:, :, 1], vg], axis=1)
            o, lse = _dilated_sample(qg, k_ext, v_ext, n_past, window, dil)
            new_kv.append(jnp.stack([k_ext, v_ext], axis=2)[:, -n_past:])
        outs.append(o)
        lses.append(lse)
    wts = jax.nn.softmax(jnp.stack(lses, axis=0), axis=0)
    o = jnp.sum(wts[..., None] * jnp.stack(outs, axis=0).astype(jnp.float32), axis=0).astype(x.dtype)
    a = o.reshape(B, T, ATTN_OUT) * jax.nn.silu(a_gate)
    p_attn = a @ w_pa

    merged = jax.nn.sigmoid(g_conv) * p_conv + jax.nn.sigmoid(g_attn) * p_attn
    return x + merged @ w_o, new_conv, new_kv


def setup_inputs(seed: int = 0) -> dict:
    key = jax.random.key(seed)
    ks = jax.random.split(key, 18)
    nrm = jax.random.normal
    f32 = jnp.float32
    def kv_cache(k, window):
        return nrm(k, (DEPTH, DEC_BATCH, min(window, PAST_LEN), 2, HEADS_PER_GROUP, HEAD_DIM), f32)
    return {
        'x_prompt': nrm(ks[0], (BATCH, SEQ, D_MODEL), f32),
        'x_sample': nrm(ks[1], (DEC_BATCH, DEC_SEQ, D_MODEL), f32),
        'cache_kv_w128': kv_cache(ks[2], GROUPS[0][0]),
        'cache_kv_w512': kv_cache(ks[3], GROUPS[1][0]),
        'cache_kv_w2048': kv_cache(ks[4], GROUPS[2][0]),
        'state_conv': 0.5 * nrm(ks[5], (DEPTH, DEC_BATCH, CONV_WIDTH - 1, CONV_DIM), f32),
        'norm_g': 1.0 + 0.02 * nrm(ks[6], (DEPTH, D_MODEL), f32),
        'w_in': nrm(ks[7], (DEPTH, D_MODEL, N_IN), f32) * D_MODEL ** -0.5,
        'dw_w': nrm(ks[8], (DEPTH, CONV_WIDTH, CONV_DIM), f32) * CONV_WIDTH ** -0.5,
        'dw_b': 0.02 * nrm(ks[9], (DEPTH, CONV_DIM), f32),
        'ln_g': 1.0 + 0.02 * nrm(ks[10], (DEPTH, CONV_DIM), f32),
        'ln_b': 0.02 * nrm(ks[11], (DEPTH, CONV_DIM), f32),
        'w_pc': nrm(ks[12], (DEPTH, CONV_DIM, D_MODEL), f32) * CONV_DIM ** -0.5,
        'w_pa': nrm(ks[13], (DEPTH, ATTN_OUT, D_MODEL), f32) * ATTN_OUT ** -0.5,
        'w_o': nrm(ks[14], (DEPTH, D_MODEL, D_MODEL), f32) * D_MODEL ** -0.5,
        'final_g': 1.0 + 0.02 * nrm(ks[15], (D_MODEL,), f32),
    }


def reference(x_prompt, x_sample, cache_kv_w128, cache_kv_w512, cache_kv_w2048, state_conv,
              norm_g, w_in, dw_w, dw_b, ln_g, ln_b, w_pc, w_pa, w_o, final_g):
    pos_p = jnp.arange(x_prompt.shape[1], dtype=jnp.int32)
    pos_s = PAST_LEN + jnp.arange(x_sample.shape[1], dtype=jnp.int32)
    hp, hs = x_prompt, x_sample
    zero_ctx = jnp.zeros((hp.shape[0], CONV_WIDTH - 1, CONV_DIM), hp.dtype)
    conv_p, conv_s = [], []
    kv_p = [[] for _ in GROUPS]
    kv_s = [[] for _ in GROUPS]
    for l in range(DEPTH):
        params = (norm_g[l], w_in[l], dw_w[l], dw_b[l], ln_g[l], ln_b[l], w_pc[l], w_pa[l], w_o[l])
        hp, cp, kvp = _layer(hp, pos_p, zero_ctx, None, *params)
        hs, cs, kvs = _layer(hs, pos_s, state_conv[l],
                             (cache_kv_w128[l], cache_kv_w512[l], cache_kv_w2048[l]), *params)
        conv_p.append(cp)
        conv_s.append(cs)
        for g in range(N_GROUPS):
            kv_p[g].append(kvp[g])
            kv_s[g].append(kvs[g])
    y_prompt = _rmsnorm(hp, final_g)
    y_sample = _rmsnorm(hs, final_g)
    return (y_prompt, y_sample,
            jnp.stack(kv_p[0]), jnp.stack(kv_p[1]), jnp.stack(kv_p[2]), jnp.stack(conv_p),
            jnp.stack(kv_s[0]), jnp.stack(kv_s[1]), jnp.stack(kv_s[2]), jnp.stack(conv_s))
```

```python
import numpy as np
import ml_dtypes
import concourse.bass as bass
import concourse.mybir as mybir
from concourse.bass_utils import run_bass_kernel_spmd

F32 = mybir.dt.float32
BF16 = mybir.dt.bfloat16
AF = mybir.ActivationFunctionType
ALU = mybir.AluOpType
NEG = -30000.0
EPS = 1e-6
CW = 31
HALO = 30
ENGS = ['pe', 'dve', 'act', 'pool', 'sp']


import os
class _Stop(Exception):
    pass
def _chk(stage):
    if os.environ.get("KSTOP") == stage:
        raise _Stop()
class Cfg:
    def __init__(self, D=4096, SEQ=2048, PAST=16384):
        self.D = D
        self.KC = D // 128
        self.SEQ = SEQ
        self.HPG = D // 512
        self.NH = 3 * self.HPG
        self.QKV = self.NH * 128
        self.AO = self.HPG * 128
        self.NIN = 3 * D + 3 * self.QKV + self.AO + 2 * D
        self.oA, self.oB, self.oCG = 0, D, 2 * D
        self.oQ = 3 * D
        self.oK = self.oQ + self.QKV
        self.oV = self.oK + self.QKV
        self.oAG = self.oV + self.QKV
        self.oGC = self.oAG + self.AO
        self.oGA = self.oGC + D
        self.T = 512
        self.NT = SEQ // self.T
        self.NP = [min(w, PAST) for w in (128, 512, 2048)]
        self.DIL = [1, 4, 16]


class Sched:
    def __init__(self, ndma=10):
        self.lists = {e: [] for e in ENGS}
        self.ndma = ndma
        self.epoch = 0
        self._reset()

    def _reset(self):
        self.cnt = {e: 0 for e in ENGS}
        self.seen = {e: {} for e in ENGS}
        self.lastw = {}
        self.readers = {}
        self.dcnt = {}
        self.drr = {e: 0 for e in ENGS}

    def _wait(self, eng, tok):
        s, v = tok
        if eng == 'pe' and s == ('eng', 'pe'):
            return
        if self.seen[eng].get(s, 0) >= v:
            return
        self.seen[eng][s] = v
        self.lists[eng].append(('w', (self.epoch,) + s, v))

    def _deps(self, eng, reads, writes):
        for b in reads:
            if b in self.lastw:
                self._wait(eng, self.lastw[b])
        for b in writes:
            if b in self.lastw:
                self._wait(eng, self.lastw[b])
            for s, v in self.readers.get(b, {}).items():
                self._wait(eng, (s, v))

    def _commit(self, tok, reads, writes):
        s, v = tok
        for b in reads:
            d = self.readers.setdefault(b, {})
            if d.get(s, 0) < v:
                d[s] = v
        for b in writes:
            self.lastw[b] = tok
            self.readers[b] = {}

    def op(self, eng, fn, reads=(), writes=(), sig=True):
        self._deps(eng, reads, writes)
        if sig:
            self.cnt[eng] += 1
            tok = (('eng', eng), self.cnt[eng])
            self.lists[eng].append(('o', fn, (self.epoch, 'eng', eng)))
        else:
            tok = (('eng', eng), self.cnt[eng] + 1)
            self.lists[eng].append(('o', fn, None))
        self._commit(tok, reads, writes)

    def dma(self, q, fn, reads=(), writes=()):
        self._deps(q, reads, writes)
        i = self.drr[q]
        self.drr[q] = (i + 1) % self.ndma
        key = ('dma', q, i)
        prev = self.dcnt.get(key, 0)
        if prev:
            self._wait(q, (key, prev))
        self.dcnt[key] = prev + 16
        tok = (key, prev + 16)
        self.lists[q].append(('d', fn, (self.epoch,) + key))
        self._commit(tok, reads, writes)

    def barrier(self):
        finals = [(('eng', e), self.cnt[e]) for e in ENGS if self.cnt[e]]
        finals += [(k, v) for k, v in self.dcnt.items()]
        for e in ENGS:
            for t in finals:
                self._wait(e, t)
        self.epoch += 1
        self._reset()


def build(cfg):
    D, KC, T, HPG = cfg.D, cfg.KC, cfg.T, cfg.HPG
    SEQ = cfg.SEQ
    nc = bass.Bass("TRN2", target_bir_lowering=False)

    def din(name, shape, dt=F32):
        return nc.dram_tensor(name, list(shape), dt, kind="ExternalInput").ap()

    def dout(name, shape, dt=F32):
        return nc.dram_tensor(name, list(shape), dt, kind="ExternalOutput").ap()

    def dscr(name, shape, dt=F32):
        return nc.dram_tensor(name, list(shape), dt, kind="Internal").ap()

    xT = din("xT", [KC, 128, SEQ])
    xsT = din("xsT", [KC, 128, 1])
    w_in = din("w_in", [2, D, cfg.NIN])
    w_pc = din("w_pc", [2, D, D])
    w_pa = din("w_pa", [2, cfg.AO, D])
    w_o = din("w_o", [2, D, D])
    vecs = din("vecs", [2, 128, 4, KC])
    fing = din("fing", [128, KC])
    dww = din("dww", [2, 128, KC, CW])
    stT = din("stT", [2, 128, KC, HALO])
    cch = [din(f"cch{g}", [2, cfg.NP[g], 2, HPG, 128]) for g in range(3)]
    ckt = [din(f"ckt{g}", [2, HPG, 128, cfg.NP[g]]) for g in range(3)]
    ropeC = din("ropeC", [128, SEQ])
    ropeS = din("ropeS", [128, SEQ])
    ropeCs = din("ropeCs", [128, 1])
    ropeSs = din("ropeSs", [128, 1])
    mk_in = [din("mk0", [128, 256], BF16), din("mk1", [128, 640], BF16), din("mk2", [128, 640], BF16)]
    cst_in = din("cst", [128, 3, 128], BF16)

    yT = dout("yT", [KC, 128, SEQ])
    ysT = dout("ysT", [KC, 128, 1])
    kTp = dout("kTp", [2, 3, HPG, 128, SEQ])
    vp = dout("vp", [2, 3, SEQ, HPG * 128])
    convp = dout("convp", [2, 128, KC, HALO])
    kvs = [dout(f"kvs{g}", [2, cfg.NP[g], 2, HPG, 128]) for g in range(3)]
    convs = dout("convs", [2, 128, KC, HALO])

    xs = [xT, dscr("xs1", [KC, 128, SEQ]), dscr("xs2", [KC, 128, SEQ])]
    xss = [xsT, dscr("xss1", [KC, 128, 1]), dscr("xss2", [KC, 128, 1])]
    kTs = dscr("kTs", [2, 3, HPG, 128, SEQ], BF16)
    Vs = dscr("Vs", [2, 3, HPG, SEQ, 128], BF16)

    S = Sched()
    import contextlib
    es = contextlib.ExitStack()
    with es:
        def sb(name, shape, dt):
            return es.enter_context(nc.sbuf_tensor(name, list(shape), dt))

        NWB = 4
        wb = sb("wb", [128, NWB, KC, 128], BF16)
        xst = sb("xst", [128, 4, T], F32)
        hT = sb("hT", [128, KC, T], BF16)
        cT = sb("cT", [128, KC, T], BF16)
        mT = sb("mT", [128, KC, T], BF16)
        aT = sb("aT", [128, HPG, T], BF16)
        uext = sb("uext", [128, HALO + T], F32)
        halo = sb("halo", [128, KC, HALO], F32)
        acc = sb("acc", [128, T], F32)
        NTMP = 10
        tmp = sb("tmp", [128, NTMP, T], F32)
        qT = sb("qT", [128, 3, T], BF16)
        kT = sb("kT", [128, 3, T], BF16)
        HMAX = max(cfg.NP[2], SEQ - T, 128)
        kTh = sb("kTh", [128, HMAX], BF16)
        Vh = sb("Vh", [128, max(HMAX // 128, 1), 128], BF16)
        vb = sb("vb", [128, 4, 3, 128], BF16)
        vf = sb("vf", [128, 4, 128], F32)
        zb = sb("zb", [128, T], BF16)
        pT = sb("pT", [128, 2, T], BF16)
        mk = [sb("mk0s", [128, 256], BF16), sb("mk1s", [128, 640], BF16), sb("mk2s", [128, 640], BF16)]
        cst = sb("csts", [128, 3, 128], BF16)
        onesf = sb("onesf", [128, 128], F32)
        cw = sb("cw", [128, KC, CW], F32)
        vec = sb("vec", [128, 4, KC], F32)
        fg = sb("fg", [128, KC], F32)
        ps = es.enter_context(nc.psum_tensor("ps", [128, 8, 512], F32))
        ident = cst[:, 0, :]
        pswap = cst[:, 1, :]
        onesb = cst[:, 2, :]

        TMPN = {}

        def tm(name):
            if name not in TMPN:
                TMPN[name] = len(TMPN) % NTMP
            return TMPN[name]
        for i, n in enumerate(['sq', 'rstd', 'sb', 'csq', 'lnm', 'lnr', 't1', 't2', 'y1', 'y2']):
            TMPN[n] = i
        TMPN.update({'ropeC': 0, 'ropeS': 2, 'kf': 3, 'sg': 4, 'rd': 9, 's1': 0, 's2': 2, 'xo': 3})

        def TM(name, Tn):
            return tmp[:, TMPN[name], 0:Tn]

        def TMK(name):
            return ('tmp', TMPN[name])

        S.dma('sp', lambda e: e.dma_start(out=cst[:], in_=cst_in[:]), [], ['cst'])
        for g in range(3):
            S.dma('sp', lambda e, g=g: e.dma_start(out=mk[g][:], in_=mk_in[g][:]), [], [('mk', g)])
        S.dma('sp', lambda e: e.dma_start(out=fg[:], in_=fing[:]), [], ['fg'])
        S.op('pool', lambda e: e.memset(onesf[:], 1.0), [], ['onesf'])

        wslot = [0]

        def load_w(src2d, kc):
            s = wslot[0]
            wslot[0] = (s + 1) % NWB
            S.dma('pool', lambda e, s=s: e.dma_start(out=wb[:, s, 0:kc, :],
                                                     in_=src2d.rearrange("(k p) c -> p k c", p=128)),
                  [], [('wb', s)])
            return s

        def mm_feat(bank, s, kc, rhs_fn, rkeys, Tn):
            for k in range(kc):
                S.op('pe', lambda e, k=k: e.matmul(ps[:, bank, 0:Tn], wb[:, s, k, :], rhs_fn(k),
                                                   start=(k == 0), stop=(k == kc - 1)),
                     [('wb', s)] + rkeys, [('ps', bank)], sig=(k == kc - 1))

        def rms_rstd(src, Tn, c0):
            for kg in range(KC // 4):
                S.dma('sp', lambda e, kg=kg: e.dma_start(
                    out=xst[:, :, 0:Tn], in_=src[kg * 4:(kg + 1) * 4, :, c0:c0 + Tn].rearrange("k p t -> p k t")),
                    [], ['xst'])
                for i in range(4):
                    k = kg * 4 + i
                    S.op('act', lambda e, i=i: e.activation(out=TM('sq', Tn), in_=xst[:, i, 0:Tn], func=AF.Square),
                         ['xst'], [TMK('sq')])
                    S.op('pe', lambda e, k=k: e.matmul(ps[:, 0, 0:Tn], onesf[:], TM('sq', Tn),
                                                       start=(k == 0), stop=(k == KC - 1)),
                         [TMK('sq'), 'onesf'], [('ps', 0)])
            S.op('dve', lambda e: e.tensor_scalar(out=TM('rstd', Tn), in0=ps[:, 0, 0:Tn], scalar1=1.0 / D,
                                                  scalar2=EPS, op0=ALU.mult, op1=ALU.add),
                 [('ps', 0)], [TMK('rstd')])
            S.op('act', lambda e: e.activation(out=TM('rstd', Tn), in_=TM('rstd', Tn), func=AF.Sqrt),
                 [TMK('rstd')], [TMK('rstd')])
            S.op('dve', lambda e: e.reciprocal(out=TM('rstd', Tn), in_=TM('rstd', Tn)),
                 [TMK('rstd')], [TMK('rstd')])

        def norm_apply(src, Tn, c0, gfn, outfn, okeyfn, extra_after=None):
            for kg in range(KC // 4):
                S.dma('sp', lambda e, kg=kg: e.dma_start(
                    out=xst[:, :, 0:Tn], in_=src[kg * 4:(kg + 1) * 4, :, c0:c0 + Tn].rearrange("k p t -> p k t")),
                    [], ['xst'])
                for i in range(4):
                    k = kg * 4 + i
                    S.op('dve', lambda e, i=i, k=k: e.scalar_tensor_tensor(
                        out=outfn(k), in0=xst[:, i, 0:Tn], scalar=gfn(k), in1=TM('rstd', Tn),
                        op0=ALU.mult, op1=ALU.mult), ['xst', TMK('rstd'), 'vec', 'fg'], [okeyfn(k)])
                    if extra_after:
                        extra_after(k)

        def run_pass(l, Tn, src, dst, c0, rC, rS, mode, j=0):
            prompt = (mode == 'p')
            TB = min(Tn, 128)
            NTB = Tn // TB
            win = w_in[l]
            rms_rstd(src, Tn, c0)
            norm_apply(src, Tn, c0, lambda k: vec[:, 0, k:k + 1], lambda k: hT[:, k, 0:Tn], lambda k: ('hT', k))
            hkeys = [('hT', k) for k in range(KC)]
            _chk('P0')
            for c in range(KC):
                sa = load_w(win[:, cfg.oA + c * 128: cfg.oA + (c + 1) * 128], KC)
                sbb = load_w(win[:, cfg.oB + c * 128: cfg.oB + (c + 1) * 128], KC)
                ba, bb = c % 2, 2 + c % 2
                mm_feat(ba, sa, KC, lambda k: hT[:, k, 0:Tn], hkeys, Tn)
                mm_feat(bb, sbb, KC, lambda k: hT[:, k, 0:Tn], hkeys, Tn)
                S.op('act', lambda e, bb=bb: e.activation(out=TM('sb', Tn), in_=ps[:, bb, 0:Tn], func=AF.Sigmoid),
                     [('ps', bb)], [TMK('sb')])
                S.op('act', lambda e, c=c: e.activation(out=uext[:, 0:HALO], in_=halo[:, c, :], func=AF.Copy),
                     [('halo', c)], ['uext'])
                S.op('dve', lambda e, ba=ba: e.tensor_tensor(out=uext[:, HALO:HALO + Tn], in0=ps[:, ba, 0:Tn],
                                                            in1=TM('sb', Tn), op=ALU.mult),
                     [('ps', ba), TMK('sb')], ['uext'])
                S.op('dve', lambda e, c=c: e.tensor_scalar(out=acc[:, 0:Tn], in0=uext[:, 0:Tn],
                                                           scalar1=cw[:, c, 0:1], scalar2=vec[:, 1, c:c + 1],
                                                           op0=ALU.mult, op1=ALU.add),
                     ['uext', 'cw', 'vec'], ['acc'])
                for kk in range(1, CW):
                    S.op('dve', lambda e, c=c, kk=kk: e.scalar_tensor_tensor(
                        out=acc[:, 0:Tn], in0=uext[:, kk:kk + Tn], scalar=cw[:, c, kk:kk + 1], in1=acc[:, 0:Tn],
                        op0=ALU.mult, op1=ALU.add), ['uext', 'cw', 'acc'], ['acc'])
                S.op('act', lambda e, c=c: e.activation(out=halo[:, c, :], in_=uext[:, Tn:Tn + HALO], func=AF.Copy),
                     ['uext'], [('halo', c)])
                S.op('act', lambda e: e.activation(out=TM('csq', Tn), in_=acc[:, 0:Tn], func=AF.Square),
                     ['acc'], [TMK('csq')])
                S.op('pe', lambda e, c=c: e.matmul(ps[:, 4, 0:Tn], onesf[:], acc[:, 0:Tn],
                                                   start=(c == 0), stop=(c == KC - 1)),
                     ['acc', 'onesf'], [('ps', 4)])
                S.op('pe', lambda e, c=c: e.matmul(ps[:, 5, 0:Tn], onesf[:], TM('csq', Tn),
                                                   start=(c == 0), stop=(c == KC - 1)),
                     [TMK('csq'), 'onesf'], [('ps', 5)])
                S.op('act', lambda e, c=c: e.activation(out=cT[:, c, 0:Tn], in_=acc[:, 0:Tn], func=AF.Copy),
                     ['acc'], [('cT', c)])
            S.op('dve', lambda e: e.tensor_scalar(out=TM('lnm', Tn), in0=ps[:, 4, 0:Tn], scalar1=1.0 / D,
                                                  scalar2=None, op0=ALU.mult), [('ps', 4)], [TMK('lnm')])
            S.op('dve', lambda e: e.tensor_tensor(out=TM('t1', Tn), in0=TM('lnm', Tn), in1=TM('lnm', Tn),
                                                  op=ALU.mult), [TMK('lnm')], [TMK('t1')])
            S.op('dve', lambda e: e.scalar_tensor_tensor(out=TM('lnr', Tn), in0=ps[:, 5, 0:Tn], scalar=1.0 / D,
                                                         in1=TM('t1', Tn), op0=ALU.mult, op1=ALU.subtract),
                 [('ps', 5), TMK('t1')], [TMK('lnr')])
            S.op('dve', lambda e: e.tensor_scalar(out=TM('lnr', Tn), in0=TM('lnr', Tn), scalar1=EPS,
                                                  scalar2=None, op0=ALU.add), [TMK('lnr')], [TMK('lnr')])
            S.op('act', lambda e: e.activation(out=TM('lnr', Tn), in_=TM('lnr', Tn), func=AF.Sqrt),
                 [TMK('lnr')], [TMK('lnr')])
            S.op('dve', lambda e: e.reciprocal(out=TM('lnr', Tn), in_=TM('lnr', Tn)), [TMK('lnr')], [TMK('lnr')])
            _chk('P1')
            for c in range(KC):
                s = load_w(win[:, cfg.oCG + c * 128: cfg.oCG + (c + 1) * 128], KC)
                b = c % 2
                mm_feat(b, s, KC, lambda k: hT[:, k, 0:Tn], hkeys, Tn)
                S.op('dve', lambda e, c=c: e.tensor_tensor(out=TM('t1', Tn), in0=cT[:, c, 0:Tn], in1=TM('lnm', Tn),
                                                           op=ALU.subtract), [('cT', c), TMK('lnm')], [TMK('t1')])
                S.op('dve', lambda e: e.tensor_tensor(out=TM('t2', Tn), in0=TM('t1', Tn), in1=TM('lnr', Tn),
                                                      op=ALU.mult), [TMK('t1'), TMK('lnr')], [TMK('t2')])
                S.op('act', lambda e, c=c: e.activation(out=TM('y1', Tn), in_=TM('t2', Tn), func=AF.Silu,
                                                        scale=vec[:, 2, c:c + 1], bias=vec[:, 3, c:c + 1]),
                     [TMK('t2'), 'vec'], [TMK('y1')])
                S.op('act', lambda e, b=b: e.activation(out=TM('y2', Tn), in_=ps[:, b, 0:Tn], func=AF.Silu),
                     [('ps', b)], [TMK('y2')])
                S.op('dve', lambda e, c=c: e.tensor_tensor(out=cT[:, c, 0:Tn], in0=TM('y1', Tn), in1=TM('y2', Tn),
                                                           op=ALU.mult), [TMK('y1'), TMK('y2')], [('cT', c)])
            ckeys = [('cT', k) for k in range(KC)]
            _chk('P2')
            S.dma('sp', lambda e: e.dma_start(out=TM('ropeC', Tn), in_=rC), [], [TMK('ropeC')])
            S.dma('sp', lambda e: e.dma_start(out=TM('ropeS', Tn), in_=rS), [], [TMK('ropeS')])
            t0 = c0
            _chk('P3r')
            for h in range(HPG):
                for qk in range(2):
                    for g in range(3):
                        col = (cfg.oQ if qk == 0 else cfg.oK) + (g * HPG + h) * 128
                        s = load_w(win[:, col:col + 128], KC)
                        b = (g % 2) if 'B' not in os.environ.get('KSKIP', '') else 0
                        mm_feat(b, s, KC, lambda k: hT[:, k, 0:Tn], hkeys, Tn)
                        S.op('act', lambda e, b=b: e.activation(out=zb[:, 0:Tn], in_=ps[:, b, 0:Tn], func=AF.Copy),
                             [('ps', b)], ['zb'])
                        if g == int(os.environ.get('KG', '0')):
                            _chk('P3z')
                        S.op('pe', lambda e: e.matmul(ps[:, 2, 0:Tn], pswap, zb[:, 0:Tn], start=True, stop=True),
                             ['zb', 'cst'], [('ps', 2)])
                        if g == int(os.environ.get('KG', '0')):
                            _chk('P3w')
                        KS = os.environ.get('KSKIP', '')
                        if not ('D' in KS and g >= 1):
                            S.op('act', lambda e, b=b: e.activation(out=TM('y1', Tn), in_=ps[:, b, 0:Tn], func=AF.Copy),
                                 [('ps', b)], [TMK('y1')])
                            S.op('dve', lambda e, b=b: e.tensor_tensor(out=TM('t1', Tn), in0=TM('y1', Tn),
                                                                      in1=TM('ropeC', Tn), op=ALU.mult),
                                 [TMK('y1'), TMK('ropeC')], [TMK('t1')])
                        if not ('E' in KS and g >= 1):
                            S.op('dve', lambda e: e.tensor_tensor(out=TM('t2', Tn), in0=ps[:, 2, 0:Tn],
                                                                  in1=TM('ropeS', Tn), op=ALU.mult),
                                 [('ps', 2), TMK('ropeS')], [TMK('t2')])
                        if g == int(os.environ.get('KG', '0')):
                            _chk('P3v')
                        if qk == 0:
                            S.op('dve', lambda e, g=g: e.tensor_tensor(out=qT[:, g, 0:Tn], in0=TM('t1', Tn),
                                                                      in1=TM('t2', Tn), op=ALU.add),
                                 [TMK('t1'), TMK('t2')], [('qT', g)])
                            if g == int(os.environ.get('KG', '0')):
                                _chk('P3q')
                        else:
                            S.op('dve', lambda e: e.tensor_tensor(out=TM('kf', Tn), in0=TM('t1', Tn),
                                                                  in1=TM('t2', Tn), op=ALU.add),
                                 [TMK('t1'), TMK('t2')], [TMK('kf')])
                            S.op('act', lambda e, g=g: e.activation(out=kT[:, g, 0:Tn], in_=TM('kf', Tn),
                                                                    func=AF.Copy), [TMK('kf')], [('kT', g)])
                            if prompt:
                                if 'a' not in os.environ.get('KSKIP', ''):
                                    S.dma('sp', lambda e, g=g, h=h: e.dma_start(out=kTp[l, g, h, :, t0:t0 + Tn],
                                                                                in_=TM('kf', Tn)), [TMK('kf')], [])
                                if 'b' not in os.environ.get('KSKIP', ''):
                                    S.dma('sp', lambda e, g=g, h=h: e.dma_start(out=kTs[l, g, h, :, t0:t0 + Tn],
                                                                                in_=kT[:, g, 0:Tn]),
                                          [('kT', g)], [('kTs', g, h)])
                            else:
                                S.dma('sp', lambda e, g=g, h=h: e.dma_start(
                                    out=kvs[g][l, cfg.NP[g] - 1, 0, h, :].rearrange("(d o) -> d o", o=1),
                                    in_=TM('kf', 1)), [TMK('kf')], [])
                _chk('P3a')
                for g in range(3):
                    col = cfg.oV + (g * HPG + h) * 128
                    s = load_w(win[:, col:col + 128], KC)
                    for tb in range(NTB):
                        for k in range(KC):
                            S.op('pe', lambda e, k=k, tb=tb, s=s: e.matmul(
                                ps[0:TB, 3, tb * 128:(tb + 1) * 128], hT[:, k, tb * TB:(tb + 1) * TB], wb[:, s, k, :],
                                start=(k == 0), stop=(k == KC - 1)),
                                [('wb', s)] + hkeys, [('ps', 3)], sig=(k == KC - 1))
                    S.op('act', lambda e: e.activation(
                        out=vf[0:TB, 0:NTB, :], in_=ps[0:TB, 3, 0:NTB * 128].rearrange("p (a b) -> p a b", b=128),
                        func=AF.Copy), [('ps', 3)], ['vf'])
                    S.op('dve', lambda e, g=g: e.tensor_copy(out=vb[0:TB, 0:NTB, g, :], in_=vf[0:TB, 0:NTB, :]),
                         ['vf'], [('vb', g)])
                    if prompt:
                        S.dma('sp', lambda e, g=g, h=h: e.dma_start(
                            out=vp[l, g, t0:t0 + Tn, h * 128:(h + 1) * 128].rearrange("(a p) d -> p a d", p=128),
                            in_=vf[:, 0:NTB, :]), ['vf'], [])
                        S.dma('sp', lambda e, g=g, h=h: e.dma_start(
                            out=Vs[l, g, h, t0:t0 + Tn, :].rearrange("(a p) d -> p a d", p=128),
                            in_=vb[:, 0:NTB, g, :]), [('vb', g)], [('Vs', g, h)])
                    else:
                        S.dma('sp', lambda e, g=g, h=h: e.dma_start(
                            out=kvs[g][l, cfg.NP[g] - 1:cfg.NP[g], 1, h, :], in_=vf[0:1, 0, :]), ['vf'], [])
                _chk('P3b')
                s = load_w(win[:, cfg.oAG + h * 128: cfg.oAG + (h + 1) * 128], KC)
                mm_feat(2, s, KC, lambda k: hT[:, k, 0:Tn], hkeys, Tn)
                S.op('act', lambda e: e.activation(out=TM('sg', Tn), in_=ps[:, 2, 0:Tn], func=AF.Silu),
                     [('ps', 2)], [TMK('sg')])
                _chk('P3c')
                gblocks = []
                gloads = []
                for g in range(3):
                    blocks = []
                    loads = []
                    if prompt:
                        nhist = [128, 512, t0][g] if j > 0 else 0
                        nhist = min(nhist, t0)
                        if nhist:
                            loads.append(('sp', lambda e, g=g, h=h, nh=nhist: e.dma_start(
                                out=kTh[:, 0:nh], in_=kTs[l, g, h, :, t0 - nh:t0]), [('kTs', g, h)], ['kTh']))
                            loads.append(('sp', lambda e, g=g, h=h, nh=nhist: e.dma_start(
                                out=Vh[:, 0:nh // 128, :],
                                in_=Vs[l, g, h, t0 - nh:t0, :].rearrange("(a p) d -> p a d", p=128)),
                                [('Vs', g, h)], ['Vh']))
                        nhb = nhist // 128
                        for i in range(nhb):
                            kb = i - nhb
                            if g == 0:
                                n0, n1, m = 0, 128, mk[0][:, 128:256]
                            elif g == 1:
                                n0, n1 = 0, min(512, 640 + 128 * kb)
                                m = mk[1][:, -128 * kb: -128 * kb + n1]
                            else:
                                n0, n1, m = 0, 512, mk[2][:, 128:640]
                            blocks.append((lambda i=i: kTh[:, i * 128:(i + 1) * 128], lambda i=i: Vh[:, i, :], 128,
                                           m, n0, n1, ['kTh', 'Vh'], g))
                        for kb in range(4):
                            if g == 0:
                                n0, n1 = 128 * kb, min(128 * kb + 256, 512)
                                m = mk[0][:, 0:n1 - n0]
                            else:
                                n0, n1 = 128 * kb, 512
                                m = mk[g][:, 0:n1 - n0]
                            blocks.append((lambda g=g, kb=kb: kT[:, g, kb * 128:(kb + 1) * 128],
                                           lambda g=g, kb=kb: vb[:, kb, g, :], 128, m, n0, n1,
                                           [('kT', g), ('vb', g)], g))
                    else:
                        npg, dil = cfg.NP[g], cfg.DIL[g]
                        loads.append(('pool', lambda e, g=g, h=h, npg=npg: e.dma_start(
                            out=kTh[:, 0:npg], in_=ckt[g][l, h]), [], ['kTh']))
                        loads.append(('pool', lambda e, g=g, h=h, npg=npg, dil=dil: e.dma_start(
                            out=Vh[:, 0, :], in_=cch[g][l, 0:npg:dil, 1, h, :]), [], ['Vh']))
                        blocks.append((lambda npg=npg, dil=dil: kTh[:, 0:npg:dil], lambda: Vh[:, 0, :], 128, None,
                                       0, 1, ['kTh', 'Vh'], g))
                        blocks.append((lambda g=g: kT[:, g, 0:1], lambda g=g: vb[0:1, 0, g, :], 1, None, 0, 1,
                                       [('kT', g), ('vb', g)], g))
                    gblocks.append(blocks)
                    gloads.append(loads)
                nb = sum(len(b) for b in gblocks)
                bi = -1
                for g in range(3):
                    for (q_, fn_, r_, w_) in gloads[g]:
                        S.dma(q_, fn_, r_, w_)
                    for (kfn, vfn, nk, m, n0, n1, rk, g_) in gblocks[g]:
                        bi += 1
                        nq = n1 - n0
                        sbk = 4 + bi % 2
                        S.op('pe', lambda e, kfn=kfn, g=g, n0=n0, n1=n1, nk=nk, nq=nq, sbk=sbk, m=m: e.matmul(
                            ps[0:nk, sbk, 0:nq], kfn(), qT[:, g, n0:n1], start=True, stop=(m is None)),
                            rk + [('qT', g)], [('ps', sbk)], sig=(m is None))
                        if m is not None:
                            S.op('pe', lambda e, nk=nk, nq=nq, sbk=sbk, m=m: e.matmul(
                                ps[0:nk, sbk, 0:nq], ident, m, start=False, stop=True),
                                ['cst', ('mk', g)], [('ps', sbk)])
                        S.op('act', lambda e, nk=nk, nq=nq, sbk=sbk, bi=bi: e.activation(
                            out=pT[0:nk, bi % 2, 0:nq], in_=ps[0:nk, sbk, 0:nq], func=AF.Exp, scale=128.0 ** -0.5),
                            [('ps', sbk)], [('pT', bi % 2)])
                        S.op('pe', lambda e, vfn=vfn, nk=nk, nq=nq, n0=n0, n1=n1, bi=bi: e.matmul(
                            ps[:, 6, n0:n1], vfn(), pT[0:nk, bi % 2, 0:nq], start=(bi == 0), stop=(bi == nb - 1)),
                            rk + [('pT', bi % 2)], [('ps', 6)], sig=False)
                        S.op('pe', lambda e, nk=nk, nq=nq, n0=n0, n1=n1, bi=bi: e.matmul(
                            ps[:, 7, n0:n1], onesb[0:nk, :], pT[0:nk, bi % 2, 0:nq], start=(bi == 0),
                            stop=(bi == nb - 1)),
                            ['cst', ('pT', bi % 2)], [('ps', 7)])
                _chk('P3d')
                S.op('dve', lambda e: e.reciprocal(out=TM('rd', Tn), in_=ps[:, 7, 0:Tn]), [('ps', 7)], [TMK('rd')])
                S.op('dve', lambda e: e.tensor_tensor(out=TM('t1', Tn), in0=ps[:, 6, 0:Tn], in1=TM('rd', Tn),
                                                      op=ALU.mult), [('ps', 6), ('ps', 7), TMK('rd')], [TMK('t1')])
                S.op('dve', lambda e, h=h: e.tensor_tensor(out=aT[:, h, 0:Tn], in0=TM('t1', Tn), in1=TM('sg', Tn),
                                                           op=ALU.mult), [TMK('t1'), TMK('sg')], [('aT', h)])
            akeys = [('aT', h) for h in range(HPG)]
            _chk('P3')
            for m_ in range(KC):
                base = 4 * (m_ % 2)
                s1 = load_w(win[:, cfg.oGC + m_ * 128: cfg.oGC + (m_ + 1) * 128], KC)
                mm_feat(base + 0, s1, KC, lambda k: hT[:, k, 0:Tn], hkeys, Tn)
                s2 = load_w(win[:, cfg.oGA + m_ * 128: cfg.oGA + (m_ + 1) * 128], KC)
                mm_feat(base + 1, s2, KC, lambda k: hT[:, k, 0:Tn], hkeys, Tn)
                s3 = load_w(w_pc[l][:, m_ * 128:(m_ + 1) * 128], KC)
                mm_feat(base + 2, s3, KC, lambda k: cT[:, k, 0:Tn], ckeys, Tn)
                s4 = load_w(w_pa[l][:, m_ * 128:(m_ + 1) * 128], HPG)
                mm_feat(base + 3, s4, HPG, lambda k: aT[:, k, 0:Tn], akeys, Tn)
                S.op('act', lambda e, base=base: e.activation(out=TM('s1', Tn), in_=ps[:, base, 0:Tn],
                                                              func=AF.Sigmoid), [('ps', base)], [TMK('s1')])
                S.op('act', lambda e, base=base: e.activation(out=TM('s2', Tn), in_=ps[:, base + 1, 0:Tn],
                                                              func=AF.Sigmoid), [('ps', base + 1)], [TMK('s2')])
                S.op('dve', lambda e, base=base: e.tensor_tensor(out=TM('t1', Tn), in0=ps[:, base + 2, 0:Tn],
                                                                in1=TM('s1', Tn), op=ALU.mult),
                     [('ps', base + 2), TMK('s1')], [TMK('t1')])
                S.op('dve', lambda e, base=base: e.tensor_tensor(out=TM('t2', Tn), in0=ps[:, base + 3, 0:Tn],
                                                                in1=TM('s2', Tn), op=ALU.mult),
                     [('ps', base + 3), TMK('s2')], [TMK('t2')])
                S.op('dve', lambda e, m_=m_: e.tensor_tensor(out=mT[:, m_, 0:Tn], in0=TM('t1', Tn), in1=TM('t2', Tn),
                                                             op=ALU.add), [TMK('t1'), TMK('t2')], [('mT', m_)])
            mkeys = [('mT', k) for k in range(KC)]
            _chk('P4')
            for kg in range(KC // 4):
                S.dma('sp', lambda e, kg=kg: e.dma_start(
                    out=xst[:, :, 0:Tn], in_=src[kg * 4:(kg + 1) * 4, :, c0:c0 + Tn].rearrange("k p t -> p k t")),
                    [], ['xst'])
                for i in range(4):
                    m_ = kg * 4 + i
                    b = m_ % 2
                    s = load_w(w_o[l][:, m_ * 128:(m_ + 1) * 128], KC)
                    mm_feat(b, s, KC, lambda k: mT[:, k, 0:Tn], mkeys, Tn)
                    S.op('dve', lambda e, b=b, i=i: e.tensor_tensor(out=TM('xo', Tn), in0=ps[:, b, 0:Tn],
                                                                   in1=xst[:, i, 0:Tn], op=ALU.add),
                         [('ps', b), 'xst'], [TMK('xo')])
                    S.dma('sp', lambda e, m_=m_: e.dma_start(out=dst[m_, :, c0:c0 + Tn], in_=TM('xo', Tn)),
                          [TMK('xo')], [('dst', m_)])

        def final_norm(src, out, Tn, c0):
            for m_ in range(KC):
                S._deps('sp', [('dst', m_)], [])
            rms_rstd(src, Tn, c0)

            def after(k):
                S.dma('sp', lambda e, k=k: e.dma_start(out=out[k, :, c0:c0 + Tn], in_=TM('xo', Tn)),
                      [TMK('xo')], [])
            norm_apply(src, Tn, c0, lambda k: fg[:, k:k + 1], lambda k: TM('xo', Tn), lambda k: TMK('xo'),
                       extra_after=after)

        try:
          for l in range(2):
              S.dma('sp', lambda e, l=l: e.dma_start(out=vec[:], in_=vecs[l]), [], ['vec'])
              S.dma('sp', lambda e, l=l: e.dma_start(out=cw[:], in_=dww[l]), [], ['cw'])
              S.op('pool', lambda e: e.memset(halo[:], 0.0), [], [('halo', c) for c in range(KC)])
              for j in range(cfg.NT):
                  run_pass(l, T, xs[l], xs[l + 1], j * T, ropeC[:, j * T:(j + 1) * T], ropeS[:, j * T:(j + 1) * T],
                           'p', j)
              S.dma('sp', lambda e, l=l: e.dma_start(out=convp[l], in_=halo[:]),
                    [('halo', c) for c in range(KC)], [])
              S.dma('sp', lambda e, l=l: e.dma_start(out=halo[:], in_=stT[l]), [], [('halo', c) for c in range(KC)])
              for g in range(3):
                  npg = cfg.NP[g]
                  for r0 in range(0, npg - 1, 256):
                      r1 = min(r0 + 256, npg - 1)
                      S.dma('sp', lambda e, l=l, g=g, r0=r0, r1=r1: e.dma_start(out=kvs[g][l, r0:r1],
                                                                             in_=cch[g][l, r0 + 1:r1 + 1]), [], [])
              run_pass(l, 1, xss[l], xss[l + 1], 0, ropeCs[:, :], ropeSs[:, :], 's')
              S.dma('sp', lambda e, l=l: e.dma_start(out=convs[l], in_=halo[:]),
                    [('halo', c) for c in range(KC)], [])
              S.barrier()
          for j in range(cfg.NT):
              final_norm(xs[2], yT, T, j * T)
          final_norm(xss[2], ysT, 1, 0)
          S.barrier()

        except _Stop:
            S.barrier()
        sem_objs = {}

        def sem(key):
            if key not in sem_objs:
                sem_objs[key] = es.enter_context(nc.semaphore("s" + str(len(sem_objs))))
            return sem_objs[key]

        for name in ENGS:
            for it in S.lists[name]:
                sem(it[1] if it[0] == 'w' else it[2]) if (it[0] == 'w' or it[2] is not None) else None
        with nc.allow_non_contiguous_dma(reason="small feature-major rows"):
            block = es.enter_context(nc.Block())

            def emit(name):
                def f(e):
                    for it in S.lists[name]:
                        if it[0] == 'w':
                            e.wait_ge(sem(it[1]), it[2])
                        elif it[0] == 'o':
                            ins = it[1](e)
                            if it[2] is not None:
                                ins.then_inc(sem(it[2]), 1)
                        else:
                            it[1](e).then_inc(sem(it[2]), 16)
                return f
            block.tensor(emit('pe'))
            block.vector(emit('dve'))
            block.scalar(emit('act'))
            block.gpsimd(emit('pool'))
            block.sync(emit('sp'))
    return nc


def _fm(v, KC):
    return np.ascontiguousarray(v.reshape(KC, 128).T)


def host_tables(cfg, past):
    half = 64
    inv = (10000.0 ** (-np.arange(half, dtype=np.float32) / half)).astype(np.float32)

    def tabs(pos):
        ang = pos.astype(np.float32)[None, :] * np.concatenate([inv, inv])[:, None]
        c = np.cos(ang).astype(np.float32)
        s = np.sin(ang).astype(np.float32)
        s[:half] *= -1.0
        return c, s
    rc, rs = tabs(np.arange(cfg.SEQ))
    rcs, rss = tabs(np.array([past]))
    p = np.arange(128)[:, None]

    def mask(width, dil, win):
        n = np.arange(width)[None, :]
        d = n - p
        ok = (d >= 0) & (d % dil == 0)
        if win is not None:
            ok &= d <= win
        return np.where(ok, 0.0, NEG).astype(ml_dtypes.bfloat16)
    mks = [mask(256, 1, 128), mask(640, 4, 512), mask(640, 16, None)]
    cst = np.zeros((128, 3, 128), np.float32)
    cst[:, 0, :] = np.eye(128)
    for m in range(128):
        cst[(m + 64) % 128, 1, m] = 1.0
    cst[:, 2, :] = 1.0
    return rc, rs, rcs, rss, mks, cst.astype(ml_dtypes.bfloat16)


_NC_CACHE = {}


def run(cfg, past, x_prompt, x_sample, cache_kv_w128, cache_kv_w512, cache_kv_w2048, state_conv,
        norm_g, w_in, dw_w, dw_b, ln_g, ln_b, w_pc, w_pa, w_o, final_g):
    KC, HPG, SEQ = cfg.KC, cfg.HPG, cfg.SEQ
    f = lambda a: np.ascontiguousarray(np.asarray(a, dtype=np.float32))
    x_prompt, x_sample = f(x_prompt), f(x_sample)
    caches = [f(cache_kv_w128), f(cache_kv_w512), f(cache_kv_w2048)]
    state_conv = f(state_conv)
    w_in, w_pc, w_pa, w_o = f(w_in), f(w_pc), f(w_pa), f(w_o)
    rc, rs, rcs, rss, mks, cst = host_tables(cfg, past)
    vecs = np.stack([np.stack([_fm(f(v)[l], KC) for v in (norm_g, dw_b, ln_g, ln_b)], axis=1) for l in range(2)])
    fing = _fm(f(final_g), KC)
    dww = np.ascontiguousarray(f(dw_w).reshape(2, CW, KC, 128).transpose(0, 3, 2, 1))
    key = (cfg.D, cfg.SEQ)
    if key not in _NC_CACHE:
        _NC_CACHE[key] = build(cfg)
    nc = _NC_CACHE[key]
    in_maps = []
    for c in range(8):
        b = c // 2
        m = {
            "xT": np.ascontiguousarray(x_prompt[b].T.reshape(KC, 128, SEQ)),
            "xsT": np.ascontiguousarray(x_sample[c].T.reshape(KC, 128, 1)),
            "w_in": w_in, "w_pc": w_pc, "w_pa": w_pa, "w_o": w_o,
            "vecs": vecs, "fing": fing, "dww": dww,
            "stT": np.ascontiguousarray(state_conv[:, c].reshape(2, HALO, KC, 128).transpose(0, 3, 2, 1)),
            "ropeC": rc, "ropeS": rs, "ropeCs": rcs, "ropeSs": rss,
            "mk0": mks[0], "mk1": mks[1], "mk2": mks[2], "cst": cst,
        }
        for g in range(3):
            m[f"cch{g}"] = np.ascontiguousarray(caches[g][:, c])
            m[f"ckt{g}"] = np.ascontiguousarray(caches[g][:, c, :, 0].transpose(0, 2, 3, 1))
        in_maps.append(m)
    res = run_bass_kernel_spmd(nc, in_maps, core_ids=list(range(8))).results
    D = cfg.D
    y_prompt = np.stack([res[2 * b]["yT"].reshape(D, SEQ).T for b in range(4)])
    y_sample = np.stack([res[c]["ysT"].reshape(D, 1).T for c in range(8)])
    kvp = []
    for g, W in enumerate((128, 512, 2048)):
        keep = min(W, SEQ)
        per_l = []
        for l in range(2):
            per_b = []
            for b in range(4):
                r = res[2 * b]
                k = r["kTp"][l, g].transpose(2, 0, 1)
                v = r["vp"][l, g].reshape(SEQ, HPG, 128)
                per_b.append(np.stack([k, v], axis=1)[SEQ - keep:])
            per_l.append(np.stack(per_b))
        kvp.append(np.stack(per_l))
    conv_p = np.stack([np.stack([res[2 * b]["convp"][l].transpose(2, 1, 0).reshape(HALO, D) for b in range(4)])
                       for l in range(2)])
    kvs = [np.stack([res[c][f"kvs{g}"] for c in range(8)], axis=1) for g in range(3)]
    conv_s = np.stack([np.stack([res[c]["convs"][l].transpose(2, 1, 0).reshape(HALO, D) for c in range(8)])
                       for l in range(2)])
    return (y_prompt.astype(np.float32), y_sample.astype(np.float32), kvp[0], kvp[1], kvp[2], conv_p,
            kvs[0], kvs[1], kvs[2], conv_s)


def kernel(**inputs):
    cfg = Cfg(4096, 2048, 16384)
    return run(cfg, 16384, **inputs)
```

```python
import numpy as np
import ml_dtypes
import concourse.bass as bass
import concourse.mybir as mybir
from concourse.bass_utils import run_bass_kernel_spmd

F32 = mybir.dt.float32
BF16 = mybir.dt.bfloat16
AF = mybir.ActivationFunctionType
ALU = mybir.AluOpType
NEG = -30000.0
EPS = 1e-6
CW = 31
HALO = 30
ENGS = ['pe', 'dve', 'act', 'pool', 'sp']


import os
class _Stop(Exception):
    pass
def _chk(stage):
    if os.environ.get("KSTOP") == stage:
        raise _Stop()
class Cfg:
    def __init__(self, D=4096, SEQ=2048, PAST=16384):
        self.D = D
        self.KC = D // 128
        self.SEQ = SEQ
        self.HPG = D // 512
        self.NH = 3 * self.HPG
        self.QKV = self.NH * 128
        self.AO = self.HPG * 128
        self.NIN = 3 * D + 3 * self.QKV + self.AO + 2 * D
        self.oA, self.oB, self.oCG = 0, D, 2 * D
        self.oQ = 3 * D
        self.oK = self.oQ + self.QKV
        self.oV = self.oK + self.QKV
        self.oAG = self.oV + self.QKV
        self.oGC = self.oAG + self.AO
        self.oGA = self.oGC + D
        self.T = 512
        self.NT = SEQ // self.T
        self.NP = [min(w, PAST) for w in (128, 512, 2048)]
        self.DIL = [1, 4, 16]


class Sched:
    def __init__(self, ndma=10):
        self.lists = {e: [] for e in ENGS}
        self.ndma = ndma
        self.epoch = 0
        self.relax = set()
        self._reset()

    def _reset(self):
        self.cnt = {e: 0 for e in ENGS}
        self.seen = {e: {} for e in ENGS}
        self.lastw = {}
        self.readers = {}
        self.dcnt = {}
        self.drr = {e: 0 for e in ENGS}

    def _wait(self, eng, tok):
        s, v = tok
        if eng == 'pe' and s == ('eng', 'pe'):
            return
        if eng in self.relax and s == ('eng', eng):
            return
        if self.seen[eng].get(s, 0) >= v:
            return
        self.seen[eng][s] = v
        self.lists[eng].append(('w', (self.epoch,) + s, v))

    def _deps(self, eng, reads, writes):
        for b in reads:
            if b in self.lastw:
                self._wait(eng, self.lastw[b])
        for b in writes:
            if b in self.lastw:
                self._wait(eng, self.lastw[b])
            for s, v in self.readers.get(b, {}).items():
                self._wait(eng, (s, v))

    def _commit(self, tok, reads, writes):
        s, v = tok
        for b in reads:
            d = self.readers.setdefault(b, {})
            if d.get(s, 0) < v:
                d[s] = v
        for b in writes:
            self.lastw[b] = tok
            self.readers[b] = {}

    def op(self, eng, fn, reads=(), writes=(), sig=True):
        self._deps(eng, reads, writes)
        if sig:
            self.cnt[eng] += 1
            tok = (('eng', eng), self.cnt[eng])
            self.lists[eng].append(('o', fn, (self.epoch, 'eng', eng)))
        else:
            tok = (('eng', eng), self.cnt[eng] + 1)
            self.lists[eng].append(('o', fn, None))
        self._commit(tok, reads, writes)

    def dma(self, q, fn, reads=(), writes=()):
        self._deps(q, reads, writes)
        i = self.drr[q]
        self.drr[q] = (i + 1) % self.ndma
        key = ('dma', q, i)
        prev = self.dcnt.get(key, 0)
        if prev:
            self._wait(q, (key, prev))
        self.dcnt[key] = prev + 16
        tok = (key, prev + 16)
        self.lists[q].append(('d', fn, (self.epoch,) + key))
        self._commit(tok, reads, writes)

    def barrier(self):
        finals = [(('eng', e), self.cnt[e]) for e in ENGS if self.cnt[e]]
        finals += [(k, v) for k, v in self.dcnt.items()]
        for e in ENGS:
            for t in finals:
                self._wait(e, t)
        self.epoch += 1
        self._reset()


def build(cfg):
    D, KC, T, HPG = cfg.D, cfg.KC, cfg.T, cfg.HPG
    SEQ = cfg.SEQ
    nc = bass.Bass("TRN2", target_bir_lowering=False)

    def din(name, shape, dt=F32):
        return nc.dram_tensor(name, list(shape), dt, kind="ExternalInput").ap()

    def dout(name, shape, dt=F32):
        return nc.dram_tensor(name, list(shape), dt, kind="ExternalOutput").ap()

    def dscr(name, shape, dt=F32):
        return nc.dram_tensor(name, list(shape), dt, kind="Internal").ap()

    xT = din("xT", [KC, 128, SEQ])
    xsT = din("xsT", [KC, 128, 1])
    w_in = din("w_in", [2, D, cfg.NIN])
    w_pc = din("w_pc", [2, D, D])
    w_pa = din("w_pa", [2, cfg.AO, D])
    w_o = din("w_o", [2, D, D])
    vecs = din("vecs", [2, 128, 4, KC])
    fing = din("fing", [128, KC])
    dww = din("dww", [2, 128, KC, CW])
    stT = din("stT", [2, 128, KC, HALO])
    cch = [din(f"cch{g}", [2, cfg.NP[g], 2, HPG, 128]) for g in range(3)]
    ckt = [din(f"ckt{g}", [2, HPG, 128, cfg.NP[g]]) for g in range(3)]
    ropeC = din("ropeC", [128, SEQ])
    ropeS = din("ropeS", [128, SEQ])
    ropeCs = din("ropeCs", [128, 1])
    ropeSs = din("ropeSs", [128, 1])
    mk_in = [din("mk0", [128, 256], BF16), din("mk1", [128, 640], BF16), din("mk2", [128, 640], BF16)]
    cst_in = din("cst", [128, 3, 128], BF16)

    yT = dout("yT", [KC, 128, SEQ])
    ysT = dout("ysT", [KC, 128, 1])
    kTp = dout("kTp", [2, 3, HPG, 128, SEQ])
    vp = dout("vp", [2, 3, SEQ, HPG * 128])
    convp = dout("convp", [2, 128, KC, HALO])
    kvs = [dout(f"kvs{g}", [2, cfg.NP[g], 2, HPG, 128]) for g in range(3)]
    convs = dout("convs", [2, 128, KC, HALO])

    xs = [xT, dscr("xs1", [KC, 128, SEQ]), dscr("xs2", [KC, 128, SEQ])]
    xss = [xsT, dscr("xss1", [KC, 128, 1]), dscr("xss2", [KC, 128, 1])]
    kTs = dscr("kTs", [2, 3, HPG, 128, SEQ], BF16)
    Vs = dscr("Vs", [2, 3, HPG, SEQ, 128], BF16)

    S = Sched()
    import contextlib
    es = contextlib.ExitStack()
    with es:
        def sb(name, shape, dt):
            return es.enter_context(nc.sbuf_tensor(name, list(shape), dt))

        NWB = 4
        wb = sb("wb", [128, NWB, KC, 128], BF16)
        xst = sb("xst", [128, 4, T], F32)
        hT = sb("hT", [128, KC, T], BF16)
        cT = sb("cT", [128, KC, T], BF16)
        mT = sb("mT", [128, KC, T], BF16)
        aT = sb("aT", [128, HPG, T], BF16)
        uext = sb("uext", [128, HALO + T], F32)
        halo = sb("halo", [128, KC, HALO], F32)
        acc = sb("acc", [128, T], F32)
        NTMP = 10
        tmp = sb("tmp", [128, NTMP, T], F32)
        qT = sb("qT", [128, 3, T], BF16)
        kT = sb("kT", [128, 3, T], BF16)
        HMAX = max(cfg.NP[2], SEQ - T, 128)
        kTh = sb("kTh", [128, HMAX], BF16)
        Vh = sb("Vh", [128, max(HMAX // 128, 1), 128], BF16)
        vb = sb("vb", [128, 4, 3, 128], BF16)
        vf = sb("vf", [128, 4, 128], F32)
        zb = sb("zb", [128, T], BF16)
        pT = sb("pT", [128, 2, T], BF16)
        mk = [sb("mk0s", [128, 256], BF16), sb("mk1s", [128, 640], BF16), sb("mk2s", [128, 640], BF16)]
        cst = sb("csts", [128, 3, 128], BF16)
        onesf = sb("onesf", [128, 128], F32)
        cw = sb("cw", [128, KC, CW], F32)
        vec = sb("vec", [128, 4, KC], F32)
        fg = sb("fg", [128, KC], F32)
        ps = es.enter_context(nc.psum_tensor("ps", [128, 8, 512], F32))
        ident = cst[:, 0, :]
        pswap = cst[:, 1, :]
        onesb = cst[:, 2, :]

        TMPN = {}

        def tm(name):
            if name not in TMPN:
                TMPN[name] = len(TMPN) % NTMP
            return TMPN[name]
        for i, n in enumerate(['sq', 'rstd', 'sb', 'csq', 'lnm', 'lnr', 't1', 't2', 'y1', 'y2']):
            TMPN[n] = i
        TMPN.update({'ropeC': 0, 'ropeS': 2, 'kf': 3, 'sg': 4, 'rd': 9, 's1': 0, 's2': 2, 'xo0': 3, 'xo1': 8})

        def TM(name, Tn):
            return tmp[:, TMPN[name], 0:Tn]

        def TMK(name):
            return ('tmp', TMPN[name])

        S.dma('sp', lambda e: e.dma_start(out=cst[:], in_=cst_in[:]), [], ['cst'])
        for g in range(3):
            S.dma('sp', lambda e, g=g: e.dma_start(out=mk[g][:], in_=mk_in[g][:]), [], [('mk', g)])
        S.dma('sp', lambda e: e.dma_start(out=fg[:], in_=fing[:]), [], ['fg'])
        S.op('pool', lambda e: e.memset(onesf[:], 1.0), [], ['onesf'])

        wslot = [0]

        def load_w(src2d, kc):
            s = wslot[0]
            wslot[0] = (s + 1) % NWB
            S.dma('pool', lambda e, s=s: e.dma_start(out=wb[:, s, 0:kc, :],
                                                     in_=src2d.rearrange("(k p) c -> p k c", p=128)),
                  [], [('wb', s)])
            return s

        def mm_feat(bank, s, kc, rhs_fn, rkeys, Tn):
            for k in range(kc):
                S.op('pe', lambda e, k=k: e.matmul(ps[:, bank, 0:Tn], wb[:, s, k, :], rhs_fn(k),
                                                   start=(k == 0), stop=(k == kc - 1)),
                     [('wb', s)] + rkeys, [('ps', bank)], sig=(k == kc - 1))

        def rms_rstd(src, Tn, c0):
            for kg in range(KC // 4):
                S.dma('sp', lambda e, kg=kg: e.dma_start(
                    out=xst[:, :, 0:Tn], in_=src[kg * 4:(kg + 1) * 4, :, c0:c0 + Tn].rearrange("k p t -> p k t")),
                    [], ['xst'])
                for i in range(4):
                    k = kg * 4 + i
                    S.op('act', lambda e, i=i: e.activation(out=TM('sq', Tn), in_=xst[:, i, 0:Tn], func=AF.Square),
                         ['xst'], [TMK('sq')])
                    S.op('pe', lambda e, k=k: e.matmul(ps[:, 0, 0:Tn], onesf[:], TM('sq', Tn),
                                                       start=(k == 0), stop=(k == KC - 1)),
                         [TMK('sq'), 'onesf'], [('ps', 0)])
            S.op('dve', lambda e: e.tensor_scalar(out=TM('rstd', Tn), in0=ps[:, 0, 0:Tn], scalar1=1.0 / D,
                                                  scalar2=EPS, op0=ALU.mult, op1=ALU.add),
                 [('ps', 0)], [TMK('rstd')])
            S.op('act', lambda e: e.activation(out=TM('rstd', Tn), in_=TM('rstd', Tn), func=AF.Sqrt),
                 [TMK('rstd')], [TMK('rstd')])
            S.op('dve', lambda e: e.reciprocal(out=TM('rstd', Tn), in_=TM('rstd', Tn)),
                 [TMK('rstd')], [TMK('rstd')])

        def norm_apply(src, Tn, c0, gfn, outfn, okeyfn, extra_after=None):
            for kg in range(KC // 4):
                S.dma('sp', lambda e, kg=kg: e.dma_start(
                    out=xst[:, :, 0:Tn], in_=src[kg * 4:(kg + 1) * 4, :, c0:c0 + Tn].rearrange("k p t -> p k t")),
                    [], ['xst'])
                for i in range(4):
                    k = kg * 4 + i
                    S.op('dve', lambda e, i=i, k=k: e.scalar_tensor_tensor(
                        out=outfn(k), in0=xst[:, i, 0:Tn], scalar=gfn(k), in1=TM('rstd', Tn),
                        op0=ALU.mult, op1=ALU.mult), ['xst', TMK('rstd'), 'vec', 'fg'], [okeyfn(k)])
                    if extra_after:
                        extra_after(k)

        def run_pass(l, Tn, src, dst, c0, rC, rS, mode, j=0):
            prompt = (mode == 'p')
            S.relax = {'dve'} if Tn >= 256 else set()
            TB = min(Tn, 128)
            NTB = Tn // TB
            win = w_in[l]
            rms_rstd(src, Tn, c0)
            norm_apply(src, Tn, c0, lambda k: vec[:, 0, k:k + 1], lambda k: hT[:, k, 0:Tn], lambda k: ('hT', k))
            hkeys = [('hT', k) for k in range(KC)]
            _chk('P0')
            for c in range(KC):
                sa = load_w(win[:, cfg.oA + c * 128: cfg.oA + (c + 1) * 128], KC)
                sbb = load_w(win[:, cfg.oB + c * 128: cfg.oB + (c + 1) * 128], KC)
                ba, bb = c % 2, 2 + c % 2
                mm_feat(ba, sa, KC, lambda k: hT[:, k, 0:Tn], hkeys, Tn)
                mm_feat(bb, sbb, KC, lambda k: hT[:, k, 0:Tn], hkeys, Tn)
                S.op('act', lambda e, bb=bb: e.activation(out=TM('sb', Tn), in_=ps[:, bb, 0:Tn], func=AF.Sigmoid),
                     [('ps', bb)], [TMK('sb')])
                S.op('act', lambda e, c=c: e.activation(out=uext[:, 0:HALO], in_=halo[:, c, :], func=AF.Copy),
                     [('halo', c)], ['uext'])
                S.op('dve', lambda e, ba=ba: e.tensor_tensor(out=uext[:, HALO:HALO + Tn], in0=ps[:, ba, 0:Tn],
                                                            in1=TM('sb', Tn), op=ALU.mult),
                     [('ps', ba), TMK('sb')], ['uext'])
                S.op('dve', lambda e, c=c: e.tensor_scalar(out=acc[:, 0:Tn], in0=uext[:, 0:Tn],
                                                           scalar1=cw[:, c, 0:1], scalar2=vec[:, 1, c:c + 1],
                                                           op0=ALU.mult, op1=ALU.add),
                     ['uext', 'cw', 'vec'], ['acc'])
                for kk in range(1, CW):
                    S.op('dve', lambda e, c=c, kk=kk: e.scalar_tensor_tensor(
                        out=acc[:, 0:Tn], in0=uext[:, kk:kk + Tn], scalar=cw[:, c, kk:kk + 1], in1=acc[:, 0:Tn],
                        op0=ALU.mult, op1=ALU.add), ['uext', 'cw', 'acc'], ['acc'])
                S.op('act', lambda e, c=c: e.activation(out=halo[:, c, :], in_=uext[:, Tn:Tn + HALO], func=AF.Copy),
                     ['uext'], [('halo', c)])
                S.op('act', lambda e: e.activation(out=TM('csq', Tn), in_=acc[:, 0:Tn], func=AF.Square),
                     ['acc'], [TMK('csq')])
                S.op('pe', lambda e, c=c: e.matmul(ps[:, 4, 0:Tn], onesf[:], acc[:, 0:Tn],
                                                   start=(c == 0), stop=(c == KC - 1)),
                     ['acc', 'onesf'], [('ps', 4)])
                S.op('pe', lambda e, c=c: e.matmul(ps[:, 5, 0:Tn], onesf[:], TM('csq', Tn),
                                                   start=(c == 0), stop=(c == KC - 1)),
                     [TMK('csq'), 'onesf'], [('ps', 5)])
                S.op('act', lambda e, c=c: e.activation(out=cT[:, c, 0:Tn], in_=acc[:, 0:Tn], func=AF.Copy),
                     ['acc'], [('cT', c)])
            S.op('dve', lambda e: e.tensor_scalar(out=TM('lnm', Tn), in0=ps[:, 4, 0:Tn], scalar1=1.0 / D,
                                                  scalar2=None, op0=ALU.mult), [('ps', 4)], [TMK('lnm')])
            S.op('dve', lambda e: e.tensor_tensor(out=TM('t1', Tn), in0=TM('lnm', Tn), in1=TM('lnm', Tn),
                                                  op=ALU.mult), [TMK('lnm')], [TMK('t1')])
            S.op('dve', lambda e: e.scalar_tensor_tensor(out=TM('lnr', Tn), in0=ps[:, 5, 0:Tn], scalar=1.0 / D,
                                                         in1=TM('t1', Tn), op0=ALU.mult, op1=ALU.subtract),
                 [('ps', 5), TMK('t1')], [TMK('lnr')])
            S.op('dve', lambda e: e.tensor_scalar(out=TM('lnr', Tn), in0=TM('lnr', Tn), scalar1=EPS,
                                                  scalar2=None, op0=ALU.add), [TMK('lnr')], [TMK('lnr')])
            S.op('act', lambda e: e.activation(out=TM('lnr', Tn), in_=TM('lnr', Tn), func=AF.Sqrt),
                 [TMK('lnr')], [TMK('lnr')])
            S.op('dve', lambda e: e.reciprocal(out=TM('lnr', Tn), in_=TM('lnr', Tn)), [TMK('lnr')], [TMK('lnr')])
            _chk('P1')
            for c in range(KC):
                s = load_w(win[:, cfg.oCG + c * 128: cfg.oCG + (c + 1) * 128], KC)
                b = c % 2
                mm_feat(b, s, KC, lambda k: hT[:, k, 0:Tn], hkeys, Tn)
                S.op('dve', lambda e, c=c: e.tensor_tensor(out=TM('t1', Tn), in0=cT[:, c, 0:Tn], in1=TM('lnm', Tn),
                                                           op=ALU.subtract), [('cT', c), TMK('lnm')], [TMK('t1')])
                S.op('dve', lambda e: e.tensor_tensor(out=TM('t2', Tn), in0=TM('t1', Tn), in1=TM('lnr', Tn),
                                                      op=ALU.mult), [TMK('t1'), TMK('lnr')], [TMK('t2')])
                S.op('act', lambda e, c=c: e.activation(out=TM('y1', Tn), in_=TM('t2', Tn), func=AF.Silu,
                                                        scale=vec[:, 2, c:c + 1], bias=vec[:, 3, c:c + 1]),
                     [TMK('t2'), 'vec'], [TMK('y1')])
                S.op('act', lambda e, b=b: e.activation(out=TM('y2', Tn), in_=ps[:, b, 0:Tn], func=AF.Silu),
                     [('ps', b)], [TMK('y2')])
                S.op('dve', lambda e, c=c: e.tensor_tensor(out=cT[:, c, 0:Tn], in0=TM('y1', Tn), in1=TM('y2', Tn),
                                                           op=ALU.mult), [TMK('y1'), TMK('y2')], [('cT', c)])
            ckeys = [('cT', k) for k in range(KC)]
            _chk('P2')
            S.dma('sp', lambda e: e.dma_start(out=TM('ropeC', Tn), in_=rC), [], [TMK('ropeC')])
            S.dma('sp', lambda e: e.dma_start(out=TM('ropeS', Tn), in_=rS), [], [TMK('ropeS')])
            t0 = c0
            _chk('P3r')
            for h in range(HPG):
                for qk in range(2):
                    for g in range(3):
                        col = (cfg.oQ if qk == 0 else cfg.oK) + (g * HPG + h) * 128
                        s = load_w(win[:, col:col + 128], KC)
                        b = (g % 2) if 'B' not in os.environ.get('KSKIP', '') else 0
                        mm_feat(b, s, KC, lambda k: hT[:, k, 0:Tn], hkeys, Tn)
                        S.op('act', lambda e, b=b: e.activation(out=zb[:, 0:Tn], in_=ps[:, b, 0:Tn], func=AF.Copy),
                             [('ps', b)], ['zb'])
                        if g == int(os.environ.get('KG', '0')):
                            _chk('P3z')
                        S.op('pe', lambda e: e.matmul(ps[:, 2, 0:Tn], pswap, zb[:, 0:Tn], start=True, stop=True),
                             ['zb', 'cst'], [('ps', 2)])
                        if g == int(os.environ.get('KG', '0')):
                            _chk('P3w')
                        KS = os.environ.get('KSKIP', '')
                        if not ('D' in KS and g >= 1):
                            S.op('act', lambda e, b=b: e.activation(out=TM('y1', Tn), in_=ps[:, b, 0:Tn], func=AF.Copy),
                                 [('ps', b)], [TMK('y1')])
                            S.op('dve', lambda e, b=b: e.tensor_tensor(out=TM('t1', Tn), in0=TM('y1', Tn),
                                                                      in1=TM('ropeC', Tn), op=ALU.mult),
                                 [TMK('y1'), TMK('ropeC')], [TMK('t1')])
                        if not ('E' in KS and g >= 1):
                            S.op('dve', lambda e: e.tensor_tensor(out=TM('t2', Tn), in0=ps[:, 2, 0:Tn],
                                                                  in1=TM('ropeS', Tn), op=ALU.mult),
                                 [('ps', 2), TMK('ropeS')], [TMK('t2')])
                        if g == int(os.environ.get('KG', '0')):
                            _chk('P3v')
                        if qk == 0:
                            S.op('dve', lambda e, g=g: e.tensor_tensor(out=qT[:, g, 0:Tn], in0=TM('t1', Tn),
                                                                      in1=TM('t2', Tn), op=ALU.add),
                                 [TMK('t1'), TMK('t2')], [('qT', g)])
                            if g == int(os.environ.get('KG', '0')):
                                _chk('P3q')
                        else:
                            S.op('dve', lambda e: e.tensor_tensor(out=TM('kf', Tn), in0=TM('t1', Tn),
                                                                  in1=TM('t2', Tn), op=ALU.add),
                                 [TMK('t1'), TMK('t2')], [TMK('kf')])
                            S.op('act', lambda e, g=g: e.activation(out=kT[:, g, 0:Tn], in_=TM('kf', Tn),
                                                                    func=AF.Copy), [TMK('kf')], [('kT', g)])
                            if prompt:
                                if 'a' not in os.environ.get('KSKIP', ''):
                                    S.dma('sp', lambda e, g=g, h=h: e.dma_start(out=kTp[l, g, h, :, t0:t0 + Tn],
                                                                                in_=TM('kf', Tn)), [TMK('kf')], [])
                                if 'b' not in os.environ.get('KSKIP', ''):
                                    S.dma('sp', lambda e, g=g, h=h: e.dma_start(out=kTs[l, g, h, :, t0:t0 + Tn],
                                                                                in_=kT[:, g, 0:Tn]),
                                          [('kT', g)], [('kTs', g, h)])
                            else:
                                S.dma('sp', lambda e, g=g, h=h: e.dma_start(
                                    out=kvs[g][l, cfg.NP[g] - 1, 0, h, :].rearrange("(d o) -> d o", o=1),
                                    in_=TM('kf', 1)), [TMK('kf')], [])
                _chk('P3a')
                for g in range(3):
                    col = cfg.oV + (g * HPG + h) * 128
                    s = load_w(win[:, col:col + 128], KC)
                    for tb in range(NTB):
                        for k in range(KC):
                            S.op('pe', lambda e, k=k, tb=tb, s=s: e.matmul(
                                ps[0:TB, 3, tb * 128:(tb + 1) * 128], hT[:, k, tb * TB:(tb + 1) * TB], wb[:, s, k, :],
                                start=(k == 0), stop=(k == KC - 1)),
                                [('wb', s)] + hkeys, [('ps', 3)], sig=(k == KC - 1))
                    S.op('act', lambda e: e.activation(
                        out=vf[0:TB, 0:NTB, :], in_=ps[0:TB, 3, 0:NTB * 128].rearrange("p (a b) -> p a b", b=128),
                        func=AF.Copy), [('ps', 3)], ['vf'])
                    S.op('dve', lambda e, g=g: e.tensor_copy(out=vb[0:TB, 0:NTB, g, :], in_=vf[0:TB, 0:NTB, :]),
                         ['vf'], [('vb', g)])
                    if prompt:
                        S.dma('sp', lambda e, g=g, h=h: e.dma_start(
                            out=vp[l, g, t0:t0 + Tn, h * 128:(h + 1) * 128].rearrange("(a p) d -> p a d", p=128),
                            in_=vf[:, 0:NTB, :]), ['vf'], [])
                        S.dma('sp', lambda e, g=g, h=h: e.dma_start(
                            out=Vs[l, g, h, t0:t0 + Tn, :].rearrange("(a p) d -> p a d", p=128),
                            in_=vb[:, 0:NTB, g, :]), [('vb', g)], [('Vs', g, h)])
                    else:
                        S.dma('sp', lambda e, g=g, h=h: e.dma_start(
                            out=kvs[g][l, cfg.NP[g] - 1:cfg.NP[g], 1, h, :], in_=vf[0:1, 0, :]), ['vf'], [])
                _chk('P3b')
                s = load_w(win[:, cfg.oAG + h * 128: cfg.oAG + (h + 1) * 128], KC)
                mm_feat(2, s, KC, lambda k: hT[:, k, 0:Tn], hkeys, Tn)
                S.op('act', lambda e: e.activation(out=TM('sg', Tn), in_=ps[:, 2, 0:Tn], func=AF.Silu),
                     [('ps', 2)], [TMK('sg')])
                _chk('P3c')
                gblocks = []
                gloads = []
                for g in range(3):
                    blocks = []
                    loads = []
                    if prompt:
                        nhist = [128, 512, t0][g] if j > 0 else 0
                        nhist = min(nhist, t0)
                        if nhist:
                            loads.append(('sp', lambda e, g=g, h=h, nh=nhist: e.dma_start(
                                out=kTh[:, 0:nh], in_=kTs[l, g, h, :, t0 - nh:t0]), [('kTs', g, h)], ['kTh']))
                            loads.append(('sp', lambda e, g=g, h=h, nh=nhist: e.dma_start(
                                out=Vh[:, 0:nh // 128, :],
                                in_=Vs[l, g, h, t0 - nh:t0, :].rearrange("(a p) d -> p a d", p=128)),
                                [('Vs', g, h)], ['Vh']))
                        nhb = nhist // 128
                        for i in range(nhb):
                            kb = i - nhb
                            if g == 0:
                                n0, n1, m = 0, 128, mk[0][:, 128:256]
                            elif g == 1:
                                n0, n1 = 0, min(512, 640 + 128 * kb)
                                m = mk[1][:, -128 * kb: -128 * kb + n1]
                            else:
                                n0, n1, m = 0, 512, mk[2][:, 128:640]
                            blocks.append((lambda i=i: kTh[:, i * 128:(i + 1) * 128], lambda i=i: Vh[:, i, :], 128,
                                           m, n0, n1, ['kTh', 'Vh'], g))
                        for kb in range(4):
                            if g == 0:
                                n0, n1 = 128 * kb, min(128 * kb + 256, 512)
                                m = mk[0][:, 0:n1 - n0]
                            else:
                                n0, n1 = 128 * kb, 512
                                m = mk[g][:, 0:n1 - n0]
                            blocks.append((lambda g=g, kb=kb: kT[:, g, kb * 128:(kb + 1) * 128],
                                           lambda g=g, kb=kb: vb[:, kb, g, :], 128, m, n0, n1,
                                           [('kT', g), ('vb', g)], g))
                    else:
                        npg, dil = cfg.NP[g], cfg.DIL[g]
                        loads.append(('pool', lambda e, g=g, h=h, npg=npg: e.dma_start(
                            out=kTh[:, 0:npg], in_=ckt[g][l, h]), [], ['kTh']))
                        loads.append(('pool', lambda e, g=g, h=h, npg=npg, dil=dil: e.dma_start(
                            out=Vh[:, 0, :], in_=cch[g][l, 0:npg:dil, 1, h, :]), [], ['Vh']))
                        blocks.append((lambda npg=npg, dil=dil: kTh[:, 0:npg:dil], lambda: Vh[:, 0, :], 128, None,
                                       0, 1, ['kTh', 'Vh'], g))
                        blocks.append((lambda g=g: kT[:, g, 0:1], lambda g=g: vb[0:1, 0, g, :], 1, None, 0, 1,
                                       [('kT', g), ('vb', g)], g))
                    gblocks.append(blocks)
                    gloads.append(loads)
                nb = sum(len(b) for b in gblocks)
                flat = []
                for g in range(3):
                    for ii, blk in enumerate(gblocks[g]):
                        flat.append((g, blk, ii == 0))

                def issue_S(bi):
                    g, (kfn, vfn, nk, m, n0, n1, rk, g_), first = flat[bi]
                    if first:
                        for (q_, fn_, r_, w_) in gloads[g]:
                            S.dma(q_, fn_, r_, w_)
                    nq = n1 - n0
                    sbk = 4 + bi % 2
                    S.op('pe', lambda e, kfn=kfn, g=g, n0=n0, n1=n1, nk=nk, nq=nq, sbk=sbk, m=m: e.matmul(
                        ps[0:nk, sbk, 0:nq], kfn(), qT[:, g, n0:n1], start=True, stop=(m is None)),
                        rk + [('qT', g)], [('ps', sbk)], sig=(m is None))
                    if m is not None:
                        S.op('pe', lambda e, nk=nk, nq=nq, sbk=sbk, m=m: e.matmul(
                            ps[0:nk, sbk, 0:nq], ident, m, start=False, stop=True),
                            ['cst', ('mk', g)], [('ps', sbk)])

                def issue_rest(bi):
                    g, (kfn, vfn, nk, m, n0, n1, rk, g_), first = flat[bi]
                    nq = n1 - n0
                    sbk = 4 + bi % 2
                    S.op('act', lambda e, nk=nk, nq=nq, sbk=sbk, bi=bi: e.activation(
                        out=pT[0:nk, bi % 2, 0:nq], in_=ps[0:nk, sbk, 0:nq], func=AF.Exp, scale=128.0 ** -0.5),
                        [('ps', sbk)], [('pT', bi % 2)])
                    S.op('pe', lambda e, vfn=vfn, nk=nk, nq=nq, n0=n0, n1=n1, bi=bi: e.matmul(
                        ps[:, 6, n0:n1], vfn(), pT[0:nk, bi % 2, 0:nq], start=(bi == 0), stop=(bi == nb - 1)),
                        rk + [('pT', bi % 2)], [('ps', 6)], sig=False)
                    S.op('pe', lambda e, nk=nk, nq=nq, n0=n0, n1=n1, bi=bi: e.matmul(
                        ps[:, 7, n0:n1], onesb[0:nk, :], pT[0:nk, bi % 2, 0:nq], start=(bi == 0),
                        stop=(bi == nb - 1)),
                        ['cst', ('pT', bi % 2)], [('ps', 7)])

                issue_S(0)
                for bi in range(nb):
                    nxt = bi + 1
                    ahead = nxt < nb and not flat[nxt][2]
                    if ahead:
                        issue_S(nxt)
                    issue_rest(bi)
                    if nxt < nb and not ahead:
                        issue_S(nxt)
                _chk('P3d')
                S.op('dve', lambda e: e.reciprocal(out=TM('rd', Tn), in_=ps[:, 7, 0:Tn]), [('ps', 7)], [TMK('rd')])
                S.op('dve', lambda e: e.tensor_tensor(out=TM('t1', Tn), in0=ps[:, 6, 0:Tn], in1=TM('rd', Tn),
                                                      op=ALU.mult), [('ps', 6), ('ps', 7), TMK('rd')], [TMK('t1')])
                S.op('dve', lambda e, h=h: e.tensor_tensor(out=aT[:, h, 0:Tn], in0=TM('t1', Tn), in1=TM('sg', Tn),
                                                           op=ALU.mult), [TMK('t1'), TMK('sg')], [('aT', h)])
            akeys = [('aT', h) for h in range(HPG)]
            _chk('P3')
            for m_ in range(KC):
                base = 4 * (m_ % 2)
                s1 = load_w(win[:, cfg.oGC + m_ * 128: cfg.oGC + (m_ + 1) * 128], KC)
                mm_feat(base + 0, s1, KC, lambda k: hT[:, k, 0:Tn], hkeys, Tn)
                s2 = load_w(win[:, cfg.oGA + m_ * 128: cfg.oGA + (m_ + 1) * 128], KC)
                mm_feat(base + 1, s2, KC, lambda k: hT[:, k, 0:Tn], hkeys, Tn)
                s3 = load_w(w_pc[l][:, m_ * 128:(m_ + 1) * 128], KC)
                mm_feat(base + 2, s3, KC, lambda k: cT[:, k, 0:Tn], ckeys, Tn)
                s4 = load_w(w_pa[l][:, m_ * 128:(m_ + 1) * 128], HPG)
                mm_feat(base + 3, s4, HPG, lambda k: aT[:, k, 0:Tn], akeys, Tn)
                S.op('act', lambda e, base=base: e.activation(out=TM('s1', Tn), in_=ps[:, base, 0:Tn],
                                                              func=AF.Sigmoid), [('ps', base)], [TMK('s1')])
                S.op('act', lambda e, base=base: e.activation(out=TM('s2', Tn), in_=ps[:, base + 1, 0:Tn],
                                                              func=AF.Sigmoid), [('ps', base + 1)], [TMK('s2')])
                S.op('dve', lambda e, base=base: e.tensor_tensor(out=TM('t1', Tn), in0=ps[:, base + 2, 0:Tn],
                                                                in1=TM('s1', Tn), op=ALU.mult),
                     [('ps', base + 2), TMK('s1')], [TMK('t1')])
                S.op('dve', lambda e, base=base: e.tensor_tensor(out=TM('t2', Tn), in0=ps[:, base + 3, 0:Tn],
                                                                in1=TM('s2', Tn), op=ALU.mult),
                     [('ps', base + 3), TMK('s2')], [TMK('t2')])
                S.op('dve', lambda e, m_=m_: e.tensor_tensor(out=mT[:, m_, 0:Tn], in0=TM('t1', Tn), in1=TM('t2', Tn),
                                                             op=ALU.add), [TMK('t1'), TMK('t2')], [('mT', m_)])
            mkeys = [('mT', k) for k in range(KC)]
            _chk('P4')
            for kg in range(KC // 4):
                S.dma('sp', lambda e, kg=kg: e.dma_start(
                    out=xst[:, :, 0:Tn], in_=src[kg * 4:(kg + 1) * 4, :, c0:c0 + Tn].rearrange("k p t -> p k t")),
                    [], ['xst'])
                for i in range(4):
                    m_ = kg * 4 + i
                    b = m_ % 2
                    s = load_w(w_o[l][:, m_ * 128:(m_ + 1) * 128], KC)
                    mm_feat(b, s, KC, lambda k: mT[:, k, 0:Tn], mkeys, Tn)
                    xo = 'xo%d' % (m_ % 2)
                    S.op('dve', lambda e, b=b, i=i, xo=xo: e.tensor_tensor(out=TM(xo, Tn), in0=ps[:, b, 0:Tn],
                                                                          in1=xst[:, i, 0:Tn], op=ALU.add),
                         [('ps', b), 'xst'], [TMK(xo)])
                    S.dma('sp', lambda e, m_=m_, xo=xo: e.dma_start(out=dst[m_, :, c0:c0 + Tn], in_=TM(xo, Tn)),
                          [TMK(xo)], [('dst', m_)])

        def final_norm(src, out, Tn, c0):
            for m_ in range(KC):
                S._deps('sp', [('dst', m_)], [])
            rms_rstd(src, Tn, c0)

            S.relax = {'dve'} if Tn >= 256 else set()

            def after(k):
                S.dma('sp', lambda e, k=k: e.dma_start(out=out[k, :, c0:c0 + Tn], in_=TM('xo%d' % (k % 2), Tn)),
                      [TMK('xo%d' % (k % 2))], [])
            norm_apply(src, Tn, c0, lambda k: fg[:, k:k + 1], lambda k: TM('xo%d' % (k % 2), Tn),
                       lambda k: TMK('xo%d' % (k % 2)), extra_after=after)

        try:
          for l in range(2):
              S.dma('sp', lambda e, l=l: e.dma_start(out=vec[:], in_=vecs[l]), [], ['vec'])
              S.dma('sp', lambda e, l=l: e.dma_start(out=cw[:], in_=dww[l]), [], ['cw'])
              S.op('pool', lambda e: e.memset(halo[:], 0.0), [], [('halo', c) for c in range(KC)])
              for j in range(cfg.NT):
                  run_pass(l, T, xs[l], xs[l + 1], j * T, ropeC[:, j * T:(j + 1) * T], ropeS[:, j * T:(j + 1) * T],
                           'p', j)
              S.dma('sp', lambda e, l=l: e.dma_start(out=convp[l], in_=halo[:]),
                    [('halo', c) for c in range(KC)], [])
              S.dma('sp', lambda e, l=l: e.dma_start(out=halo[:], in_=stT[l]), [], [('halo', c) for c in range(KC)])
              for g in range(3):
                  npg = cfg.NP[g]
                  for r0 in range(0, npg - 1, 256):
                      r1 = min(r0 + 256, npg - 1)
                      S.dma('sp', lambda e, l=l, g=g, r0=r0, r1=r1: e.dma_start(out=kvs[g][l, r0:r1],
                                                                             in_=cch[g][l, r0 + 1:r1 + 1]), [], [])
              run_pass(l, 1, xss[l], xss[l + 1], 0, ropeCs[:, :], ropeSs[:, :], 's')
              S.dma('sp', lambda e, l=l: e.dma_start(out=convs[l], in_=halo[:]),
                    [('halo', c) for c in range(KC)], [])
              S.barrier()
          for j in range(cfg.NT):
              final_norm(xs[2], yT, T, j * T)
          final_norm(xss[2], ysT, 1, 0)
          S.barrier()

        except _Stop:
            S.barrier()
        sem_objs = {}

        def sem(key):
            if key not in sem_objs:
                sem_objs[key] = es.enter_context(nc.semaphore("s" + str(len(sem_objs))))
            return sem_objs[key]

        for name in ENGS:
            for it in S.lists[name]:
                sem(it[1] if it[0] == 'w' else it[2]) if (it[0] == 'w' or it[2] is not None) else None
        with nc.allow_non_contiguous_dma(reason="small feature-major rows"):
            block = es.enter_context(nc.Block())

            def emit(name):
                def f(e):
                    for it in S.lists[name]:
                        if it[0] == 'w':
                            e.wait_ge(sem(it[1]), it[2])
                        elif it[0] == 'o':
                            ins = it[1](e)
                            if it[2] is not None:
                                ins.then_inc(sem(it[2]), 1)
                        else:
                            it[1](e).then_inc(sem(it[2]), 16)
                return f
            block.tensor(emit('pe'))
            block.vector(emit('dve'))
            block.scalar(emit('act'))
            block.gpsimd(emit('pool'))
            block.sync(emit('sp'))
    return nc


def _fm(v, KC):
    return np.ascontiguousarray(v.reshape(KC, 128).T)


def host_tables(cfg, past):
    half = 64
    inv = (10000.0 ** (-np.arange(half, dtype=np.float32) / half)).astype(np.float32)

    def tabs(pos):
        ang = pos.astype(np.float32)[None, :] * np.concatenate([inv, inv])[:, None]
        c = np.cos(ang).astype(np.float32)
        s = np.sin(ang).astype(np.float32)
        s[:half] *= -1.0
        return c, s
    rc, rs = tabs(np.arange(cfg.SEQ))
    rcs, rss = tabs(np.array([past]))
    p = np.arange(128)[:, None]

    def mask(width, dil, win):
        n = np.arange(width)[None, :]
        d = n - p
        ok = (d >= 0) & (d % dil == 0)
        if win is not None:
            ok &= d <= win
        return np.where(ok, 0.0, NEG).astype(ml_dtypes.bfloat16)
    mks = [mask(256, 1, 128), mask(640, 4, 512), mask(640, 16, None)]
    cst = np.zeros((128, 3, 128), np.float32)
    cst[:, 0, :] = np.eye(128)
    for m in range(128):
        cst[(m + 64) % 128, 1, m] = 1.0
    cst[:, 2, :] = 1.0
    return rc, rs, rcs, rss, mks, cst.astype(ml_dtypes.bfloat16)


_NC_CACHE = {}


def run(cfg, past, x_prompt, x_sample, cache_kv_w128, cache_kv_w512, cache_kv_w2048, state_conv,
        norm_g, w_in, dw_w, dw_b, ln_g, ln_b, w_pc, w_pa, w_o, final_g):
    KC, HPG, SEQ = cfg.KC, cfg.HPG, cfg.SEQ
    f = lambda a: np.ascontiguousarray(np.asarray(a, dtype=np.float32))
    x_prompt, x_sample = f(x_prompt), f(x_sample)
    caches = [f(cache_kv_w128), f(cache_kv_w512), f(cache_kv_w2048)]
    state_conv = f(state_conv)
    w_in, w_pc, w_pa, w_o = f(w_in), f(w_pc), f(w_pa), f(w_o)
    rc, rs, rcs, rss, mks, cst = host_tables(cfg, past)
    vecs = np.stack([np.stack([_fm(f(v)[l], KC) for v in (norm_g, dw_b, ln_g, ln_b)], axis=1) for l in range(2)])
    fing = _fm(f(final_g), KC)
    dww = np.ascontiguousarray(f(dw_w).reshape(2, CW, KC, 128).transpose(0, 3, 2, 1))
    key = (cfg.D, cfg.SEQ)
    if key not in _NC_CACHE:
        _NC_CACHE[key] = build(cfg)
    nc = _NC_CACHE[key]
    in_maps = []
    for c in range(8):
        b = c // 2
        m = {
            "xT": np.ascontiguousarray(x_prompt[b].T.reshape(KC, 128, SEQ)),
            "xsT": np.ascontiguousarray(x_sample[c].T.reshape(KC, 128, 1)),
            "w_in": w_in, "w_pc": w_pc, "w_pa": w_pa, "w_o": w_o,
            "vecs": vecs, "fing": fing, "dww": dww,
            "stT": np.ascontiguousarray(state_conv[:, c].reshape(2, HALO, KC, 128).transpose(0, 3, 2, 1)),
            "ropeC": rc, "ropeS": rs, "ropeCs": rcs, "ropeSs": rss,
            "mk0": mks[0], "mk1": mks[1], "mk2": mks[2], "cst": cst,
        }
        for g in range(3):
            m[f"cch{g}"] = np.ascontiguousarray(caches[g][:, c])
            m[f"ckt{g}"] = np.ascontiguousarray(caches[g][:, c, :, 0].transpose(0, 2, 3, 1))
        in_maps.append(m)
    res = run_bass_kernel_spmd(nc, in_maps, core_ids=list(range(8))).results
    D = cfg.D
    y_prompt = np.stack([res[2 * b]["yT"].reshape(D, SEQ).T for b in range(4)])
    y_sample = np.stack([res[c]["ysT"].reshape(D, 1).T for c in range(8)])
    kvp = []
    for g, W in enumerate((128, 512, 2048)):
        keep = min(W, SEQ)
        per_l = []
        for l in range(2):
            per_b = []
            for b in range(4):
                r = res[2 * b]
                k = r["kTp"][l, g].transpose(2, 0, 1)
                v = r["vp"][l, g].reshape(SEQ, HPG, 128)
                per_b.append(np.stack([k, v], axis=1)[SEQ - keep:])
            per_l.append(np.stack(per_b))
        kvp.append(np.stack(per_l))
    conv_p = np.stack([np.stack([res[2 * b]["convp"][l].transpose(2, 1, 0).reshape(HALO, D) for b in range(4)])
                       for l in range(2)])
    kvs = [np.stack([res[c][f"kvs{g}"] for c in range(8)], axis=1) for g in range(3)]
    conv_s = np.stack([np.stack([res[c]["convs"][l].transpose(2, 1, 0).reshape(HALO, D) for c in range(8)])
                       for l in range(2)])
    return (y_prompt.astype(np.float32), y_sample.astype(np.float32), kvp[0], kvp[1], kvp[2], conv_p,
            kvs[0], kvs[1], kvs[2], conv_s)


def kernel(**inputs):
    cfg = Cfg(4096, 2048, 16384)
    return run(cfg, 16384, **inputs)
```
